# Optimizing a Trainium2 kernel written in Bass

```python
import jax, jax.numpy as jnp
from jax import lax
import numpy as np

D_MODEL = 1024
BATCH = 1
SEQ = 16384
DEPTH = 2

N_MIXERS = 2
EXPAND = 2
D_INNER = EXPAND * D_MODEL
RMS_EPS = 1e-6

GLA_HEADS = 4
GLA_DK = D_MODEL // 2
GLA_DV = D_INNER
GLA_HEAD_K = GLA_DK // GLA_HEADS
GLA_HEAD_V = GLA_DV // GLA_HEADS
GLA_GATE_RANK = 16
GLA_GATE_NORMALIZER = 16.0
GLA_CHUNK = 64
GLA_PROJ = 2 * GLA_DK + 2 * GLA_DV + GLA_GATE_RANK

SSD_HEAD_DIM = 64
SSD_HEADS = D_INNER // SSD_HEAD_DIM
SSD_GROUPS = 8
SSD_HEADS_PER_GROUP = SSD_HEADS // SSD_GROUPS
SSD_STATE = 128
SSD_CONV = 4
SSD_CHUNK = 64
SSD_CONV_DIM = D_INNER + 2 * SSD_GROUPS * SSD_STATE
SSD_PROJ = D_INNER + SSD_CONV_DIM + SSD_HEADS

N_GLA = (DEPTH + 1) // 2
N_SSD = DEPTH // 2

kernel_name = "hybrid_gla_mamba2_interleaved"


def rmsnorm(x, w):
    xf = x.astype(jnp.float32)
    return xf * lax.rsqrt(jnp.mean(xf * xf, axis=-1, keepdims=True) + RMS_EPS) * w.astype(jnp.float32)


def gla_mixer(h, w_in, w_gate_up, b_gate_up, w_head_norm, w_out):
    bsz, L, _ = h.shape
    nc = L // GLA_CHUNK
    proj = h @ w_in
    q, k, v, g, gk_low = jnp.split(
        proj, [GLA_DK, 2 * GLA_DK, 2 * GLA_DK + GLA_DV, 2 * GLA_DK + 2 * GLA_DV], axis=-1)
    log_a = jax.nn.log_sigmoid((gk_low @ w_gate_up + b_gate_up).astype(jnp.float32)) / GLA_GATE_NORMALIZER

    def to_chunks(t, d):
        return t.astype(jnp.float32).reshape(bsz, nc, GLA_CHUNK, GLA_HEADS, d).transpose(1, 0, 3, 2, 4)

    qc = to_chunks(q, GLA_HEAD_K) * (GLA_HEAD_K ** -0.5)
    kc = to_chunks(k, GLA_HEAD_K)
    vc = to_chunks(v, GLA_HEAD_V)
    ac = to_chunks(log_a, GLA_HEAD_K)
    causal = jnp.tril(jnp.ones((GLA_CHUNK, GLA_CHUNK), dtype=bool))[None, None, :, :, None]

    def step(S, inp):
        qi, ki, vi, ai = inp
        b = jnp.cumsum(ai, axis=-2)
        o_inter = jnp.einsum('bhcd,bhdv->bhcv', qi * jnp.exp(b), S)
        rel = jnp.where(causal, b[:, :, :, None, :] - b[:, :, None, :, :], -jnp.inf)
        attn = jnp.sum(qi[:, :, :, None, :] * ki[:, :, None, :, :] * jnp.exp(rel), axis=-1)
        o = o_inter + jnp.einsum('bhij,bhjv->bhiv', attn, vi)
        b_last = b[:, :, -1:, :]
        S = jnp.exp(b_last[:, :, 0, :, None]) * S + jnp.einsum(
            'bhjd,bhjv->bhdv', ki * jnp.exp(b_last - b), vi)
        return S, o

    S0 = jnp.zeros((bsz, GLA_HEADS, GLA_HEAD_K, GLA_HEAD_V), jnp.float32)
    _, o = lax.scan(step, S0, (qc, kc, vc, ac))
    o = o.transpose(1, 0, 3, 2, 4).reshape(bsz, L, GLA_HEADS, GLA_HEAD_V)
    o = rmsnorm(o, w_head_norm).reshape(bsz, L, GLA_DV) * jax.nn.silu(g.astype(jnp.float32))
    return o.astype(h.dtype) @ w_out


def ssd_mixer(h, w_in, conv_w, conv_b, dt_bias, a_log, d_skip, w_gate_norm, w_out):
    bsz, L, _ = h.shape
    nc = L // SSD_CHUNK
    G, HG, P, N, C = SSD_GROUPS, SSD_HEADS_PER_GROUP, SSD_HEAD_DIM, SSD_STATE, SSD_CHUNK
    proj = h @ w_in
    z, xbc, dt = jnp.split(proj, [D_INNER, D_INNER + SSD_CONV_DIM], axis=-1)
    xbc = lax.conv_general_dilated(
        xbc, conv_w[:, None, :].astype(xbc.dtype), window_strides=(1,), padding=[(SSD_CONV - 1, 0)],
        dimension_numbers=('NWC', 'WIO', 'NWC'), feature_group_count=SSD_CONV_DIM)
    xbc = jax.nn.silu(xbc + conv_b)
    xs, Bm, Cm = jnp.split(xbc, [D_INNER, D_INNER + G * N], axis=-1)
    xs = xs.astype(jnp.float32).reshape(bsz, nc, C, G, HG, P)
    Bm = Bm.astype(jnp.float32).reshape(bsz, nc, C, G, N)
    Cm = Cm.astype(jnp.float32).reshape(bsz, nc, C, G, N)
    dt = jax.nn.softplus(dt.astype(jnp.float32) + dt_bias.astype(jnp.float32)).reshape(bsz, nc, C, G, HG)
    A = -jnp.exp(a_log.astype(jnp.float32)).reshape(G, HG)
    a_cum = jnp.cumsum(jnp.moveaxis(dt * A, 2, -1), axis=-1)
    xdt = xs * dt[..., None]

    causal = jnp.tril(jnp.ones((C, C), dtype=bool))
    Lmat = jnp.exp(jnp.where(causal, a_cum[..., :, None] - a_cum[..., None, :], -jnp.inf))
    cb = jnp.einsum('bzlgn,bzsgn->bzgls', Cm, Bm)
    y_diag = jnp.einsum('bzghls,bzsghp->bzlghp', cb[:, :, :, None] * Lmat, xdt)

    decay_to_end = jnp.moveaxis(jnp.exp(a_cum[..., -1:] - a_cum), -1, 2)
    states = jnp.einsum('bzsgn,bzsghp->bzghpn', Bm, xdt * decay_to_end[..., None])
    chunk_decay = jnp.exp(a_cum[..., -1])

    def step(hs, inp):
        st, dec = inp
        return dec[..., None, None] * hs + st, hs

    h0 = jnp.zeros((bsz, G, HG, P, N), jnp.float32)
    _, h_in = lax.scan(step, h0, (jnp.moveaxis(states, 1, 0), jnp.moveaxis(chunk_decay, 1, 0)))
    h_in = jnp.moveaxis(h_in, 0, 1)
    decay_from_start = jnp.moveaxis(jnp.exp(a_cum), -1, 2)
    y_off = jnp.einsum('bzlgn,bzghpn->bzlghp', Cm, h_in) * decay_from_start[..., None]

    y = y_diag + y_off + xs * d_skip.astype(jnp.float32).reshape(G, HG)[..., None]
    y = y.reshape(bsz, L, D_INNER) * jax.nn.silu(z.astype(jnp.float32))
    y = rmsnorm(y, w_gate_norm)
    return y.astype(h.dtype) @ w_out


def setup_inputs(seed: int = 0) -> dict:
    key = jax.random.key(seed)
    ks = jax.random.split(key, 17)
    f32 = jnp.float32

    def normal(k, shape, scale):
        return jax.random.normal(k, shape, f32) * scale

    x = jax.random.normal(ks[0], (BATCH, SEQ, D_MODEL), f32)
    norm_w = 1.0 + normal(ks[1], (DEPTH, D_MODEL), 0.02)
    gla_in_proj = normal(ks[2], (N_GLA, D_MODEL, GLA_PROJ), D_MODEL ** -0.5)
    gla_gate_up = normal(ks[3], (N_GLA, GLA_GATE_RANK, GLA_DK), GLA_GATE_RANK ** -0.5)
    gla_gate_bias = normal(ks[4], (N_GLA, GLA_DK), 0.1)
    gla_head_norm = 1.0 + normal(ks[5], (N_GLA, GLA_HEAD_V), 0.02)
    gla_out_proj = normal(ks[6], (N_GLA, GLA_DV, D_MODEL), GLA_DV ** -0.5)
    ssd_in_proj = normal(ks[7], (N_SSD, D_MODEL, SSD_PROJ), D_MODEL ** -0.5)
    ssd_conv_w = normal(ks[8], (N_SSD, SSD_CONV, SSD_CONV_DIM), SSD_CONV ** -0.5)
    ssd_conv_b = normal(ks[9], (N_SSD, SSD_CONV_DIM), 0.02)
    dt0 = jnp.exp(jax.random.uniform(ks[10], (N_SSD, SSD_HEADS), f32, np.log(1e-3), np.log(1e-1)))
    ssd_dt_bias = dt0 + jnp.log(-jnp.expm1(-dt0))
    ssd_a_log = jnp.log(jax.random.uniform(ks[11], (N_SSD, SSD_HEADS), f32, 1.0, 16.0))
    ssd_d = 1.0 + normal(ks[12], (N_SSD, SSD_HEADS), 0.02)
    ssd_gate_norm = 1.0 + normal(ks[13], (N_SSD, D_INNER), 0.02)
    ssd_out_proj = normal(ks[14], (N_SSD, D_INNER, D_MODEL), D_INNER ** -0.5)
    final_norm = 1.0 + normal(ks[15], (D_MODEL,), 0.02)
    return {"x": x, "norm_w": norm_w,
            "gla_in_proj": gla_in_proj, "gla_gate_up": gla_gate_up, "gla_gate_bias": gla_gate_bias,
            "gla_head_norm": gla_head_norm, "gla_out_proj": gla_out_proj,
            "ssd_in_proj": ssd_in_proj, "ssd_conv_w": ssd_conv_w, "ssd_conv_b": ssd_conv_b,
            "ssd_dt_bias": ssd_dt_bias, "ssd_a_log": ssd_a_log, "ssd_d": ssd_d,
            "ssd_gate_norm": ssd_gate_norm, "ssd_out_proj": ssd_out_proj,
            "final_norm": final_norm}


def reference(x, norm_w, gla_in_proj, gla_gate_up, gla_gate_bias, gla_head_norm, gla_out_proj,
              ssd_in_proj, ssd_conv_w, ssd_conv_b, ssd_dt_bias, ssd_a_log, ssd_d,
              ssd_gate_norm, ssd_out_proj, final_norm):
    for i in range(DEPTH):
        hn = rmsnorm(x, norm_w[i]).astype(x.dtype)
        j = i // N_MIXERS
        if i % N_MIXERS == 0:
            y = gla_mixer(hn, gla_in_proj[j], gla_gate_up[j], gla_gate_bias[j], gla_head_norm[j],
                          gla_out_proj[j])
        else:
            y = ssd_mixer(hn, ssd_in_proj[j], ssd_conv_w[j], ssd_conv_b[j], ssd_dt_bias[j],
                          ssd_a_log[j], ssd_d[j], ssd_gate_norm[j], ssd_out_proj[j])
        x = x + y
    return rmsnorm(x, final_norm).astype(x.dtype)
```

```python
import numpy as np
import concourse.bass as bass
import concourse.mybir as mybir
from concourse.bass_utils import run_bass_kernel_spmd

F32 = mybir.dt.float32
BF16 = mybir.dt.bfloat16
AF = mybir.ActivationFunctionType
ALU = mybir.AluOpType

NCORES = 8
SEQ = 16384
D = 1024
T = SEQ // NCORES
NT = T // 128
EPS = 1e-6
STAGE = 2
import os
DBG = int(os.environ.get('KDBG', '0'))
STAGE = int(os.environ.get('KSTAGE', '2'))
KOPT = os.environ.get('KOPT', '')


class Buf:
    __slots__ = ("name", "w", "r", "dsem", "dval", "dlast", "excl")

    def __init__(self, name, excl=False):
        self.name = name
        self.excl = excl
        self.w = None
        self.r = {}
        self.dsem = None
        self.dval = 0
        self.dlast = None


class Op:
    __slots__ = ("eng", "fn", "deps", "is_dma", "sem", "val", "ms", "idx", "tag", "inc", "nobar", "sembuf")

    def __init__(self, eng, fn, is_dma, tag):
        self.eng = eng
        self.fn = fn
        self.deps = []
        self.is_dma = is_dma
        self.sem = None
        self.val = 0
        self.ms = False
        self.idx = -1
        self.tag = tag
        self.inc = 16
        self.nobar = False
        self.sembuf = None


class Prog:
    ENGS = ("sp", "act", "dve", "pool", "pe")

    def __init__(self, nc):
        self.nc = nc
        self.ops = []
        self.streams = {e: [] for e in self.ENGS}
        self.esem = {}
        self.nsem = 0
        self.bar_dma_from = 0
        self.free_dsems = []

    def new_sem(self, name):
        self.nsem += 1
        return self.nc.alloc_semaphore(f"{name}_{self.nsem}")

    def _assign_dsem(self, sembuf):
        if sembuf.dsem is None:
            if self.free_dsems:
                sembuf.dsem, sembuf.dval = self.free_dsems.pop()
            else:
                sembuf.dsem = self.new_sem("d_" + sembuf.name)
                sembuf.dval = 0
            sembuf.dlast = None

    def _add_dep(self, op, p):
        if p is None or p is op:
            return
        if (not p.is_dma) and (not op.is_dma) and p.eng == op.eng and p.eng == "pe":
            return
        op.deps.append(p)

    def _deps(self, o, reads, writes):
        for b in reads:
            self._add_dep(o, b.w)
            if b.excl:
                for p in b.r.values():
                    if p.eng != o.eng:
                        self._add_dep(o, p)
        for b in writes:
            self._add_dep(o, b.w)
            for p in b.r.values():
                self._add_dep(o, p)
        for b in reads:
            key = id(o) if o.is_dma else o.eng
            b.r[key] = o
        for b in writes:
            b.w = o
            b.r = {}
        o.idx = len(self.ops)
        self.ops.append(o)
        self.streams[o.eng].append(o)

    def op(self, eng, fn, reads=(), writes=(), tag=""):
        o = Op(eng, fn, False, tag)
        self._deps(o, reads, writes)
        return o

    def dma(self, eng, out, in_, reads=(), writes=(), sembuf=None, chain=True, tag="", nobar=False, **kw):
        def fn(e):
            return e.dma_start(out=out, in_=in_, **kw)
        o = Op(eng, fn, True, tag)
        o.nobar = nobar
        if sembuf is None:
            sembuf = (list(writes) + list(reads))[0]
        self._assign_dsem(sembuf)
        o.sembuf = sembuf
        o.sem = sembuf.dsem
        sembuf.dval += 16
        o.val = sembuf.dval
        if chain and sembuf.dlast is not None:
            o.deps.append(sembuf.dlast)
        sembuf.dlast = o
        self._deps(o, reads, writes)
        return o

    def custom_async(self, eng, fn, sembuf, inc, reads=(), writes=(), tag=""):
        o = Op(eng, fn, True, tag)
        o.inc = inc
        self._assign_dsem(sembuf)
        o.sembuf = sembuf
        o.sem = sembuf.dsem
        sembuf.dval += inc
        o.val = sembuf.dval
        if sembuf.dlast is not None:
            o.deps.append(sembuf.dlast)
        sembuf.dlast = o
        self._deps(o, reads, writes)
        return o

    def barrier(self):
        deps = []
        for e in self.ENGS:
            last = None
            for o in reversed(self.streams[e]):
                if not o.is_dma and o.fn is not None:
                    last = o
                    break
            if last is not None:
                deps.append(last)
        deps += [o for o in self.ops if o.is_dma and not getattr(o, "nobar", False)][self.bar_dma_from:]
        self.bar_dma_from = len([o for o in self.ops if o.is_dma and not getattr(o, "nobar", False)])
        for d_ in deps:
            if d_.is_dma and d_.sembuf is not None and d_.sembuf.dlast is d_ and d_.sembuf.dsem is not None:
                self.free_dsems.append((d_.sembuf.dsem, d_.sembuf.dval))
                d_.sembuf.dsem = None
        for e in self.ENGS:
            o = Op(e, None, False, "barrier")
            o.deps = list(deps)
            o.idx = len(self.ops)
            self.ops.append(o)
            self.streams[e].append(o)

    def final_wait(self, eng, ops):
        o = Op(eng, None, False, "final")
        for p in ops:
            o.deps.append(p)
        o.idx = len(self.ops)
        self.ops.append(o)
        self.streams[eng].append(o)
        return o

    def emit(self):
        nc = self.nc
        for o in self.ops:
            for p in o.deps:
                if not p.is_dma:
                    p.ms = True
        for e in self.ENGS:
            cnt = 0
            for o in self.streams[e]:
                if o.is_dma or o.fn is None:
                    continue
                if o.ms:
                    cnt += 1
                    o.val = cnt
            if cnt > 0:
                self.esem[e] = self.new_sem("e_" + e)
        for e in self.ENGS:
            for o in self.streams[e]:
                if not o.is_dma:
                    o.sem = self.esem.get(e)
        handles = {"sp": "sync", "act": "scalar", "dve": "vector", "pool": "gpsimd", "pe": "tensor"}
        prog = self

        def run_stream(e, eng):
            waited = {}
            for o in prog.streams[e]:
                need = {}
                for p in o.deps:
                    k = id(p.sem)
                    if k not in need or need[k][1] < p.val:
                        need[k] = (p.sem, p.val)
                for k, (sem, val) in need.items():
                    if waited.get(k, 0) >= val:
                        continue
                    eng.wait_ge(sem, val)
                    waited[k] = val
                if o.fn is None:
                    continue
                ins = o.fn(eng)
                if o.is_dma:
                    ins.then_inc(o.sem, o.inc)
                elif o.ms:
                    ins.then_inc(o.sem, 1)

        with nc.Block() as block:
            for e in self.ENGS:
                if not self.streams[e]:
                    continue
                deco = getattr(block, handles[e])

                def make(e):
                    def body(eng):
                        run_stream(e, eng)
                    return body
                deco(make(e))


class Arena:
    def __init__(self, nc, nbytes):
        self.base = nc.alloc_sbuf_tensor("arena", [128, nbytes // 2], BF16).ap()
        self.nbytes = nbytes
        self.off = 0

    def reset(self):
        self.off = 0

    def alloc(self, shape, dtype):
        esz = 4 if dtype == F32 else 2
        n = 1
        for d_ in shape[1:]:
            n *= d_
        nb = (n * esz + 31) // 32 * 32
        assert self.off + nb <= self.nbytes, ("arena overflow", self.off, nb, self.nbytes)
        ap = self.base[0:shape[0], self.off // 2:(self.off + n * esz) // 2]
        if dtype == F32:
            ap = ap.bitcast(F32)
        self.off += nb
        if len(shape) == 3:
            ap = ap.rearrange("p (a b) -> p a b", a=shape[1])
        return ap


class Ring:
    def __init__(self, arena, name, shape, dtype, n):
        self.aps = [arena.alloc(list(shape), dtype) for i in range(n)]
        self.bufs = [Buf(f"{name}{i}") for i in range(n)]
        self.i = 0

    def next(self):
        k = self.i % len(self.aps)
        self.i += 1
        return self.aps[k], self.bufs[k]


def build(stage=2):
    nc = bass.Bass("TRN2", target_bir_lowering=False)
    P = Prog(nc)

    def din(name, shape):
        return nc.dram_tensor(name, list(shape), F32, kind="ExternalInput").ap()

    x_d = din("x", [T, D])
    norm_w_d = din("norm_w", [2, D])
    gla_in_d = din("gla_in_proj", [D, 5136])
    gla_up_d = din("gla_gate_up", [16, 512])
    gla_gb_d = din("gla_gate_bias", [1, 512])
    gla_hn_d = din("gla_head_norm", [1, 512])
    gla_out_d = din("gla_out_proj", [2048, D])
    rmask_d = din("rmask", [128, 16])
    ssd_in_d = din("ssd_in_proj", [D, 6176])
    ssd_cw_d = din("ssd_conv_w", [4, 4096])
    ssd_cb_d = din("ssd_conv_b", [1, 4096])
    ssd_dtb_d = din("ssd_dt_bias", [1, 32])
    ssd_alog_d = din("ssd_a_log", [1, 32])
    ssd_d_d = din("ssd_d", [1, 32])
    ssd_gn_d = din("ssd_gate_norm", [16, 128])
    ssd_out_d = din("ssd_out_proj", [2048, D])
    fnw_d = din("final_norm", [1, D])
    out_d = nc.dram_tensor("out", [T, D], F32, kind="ExternalOutput").ap()

    dkind = dict(kind="ExternalOutput") if DBG else {}
    oloc_d = nc.dram_tensor("oloc", [NT, 128, 2048], F32, **dkind).ap()
    qg_d = nc.dram_tensor("qg", [NT, 128, 512], BF16).ap()
    x1_d = nc.dram_tensor("x1s", [T, D], F32).ap()
    CCW = 2080
    ccin_d = nc.dram_tensor("ccin", [128, CCW], F32).ap()
    ccout_d = nc.dram_tensor("ccout", [NCORES * 128, CCW], F32).ap()
    cc2in_d = nc.dram_tensor("cc2in", [128, 96], F32).ap()
    cc2out_d = nc.dram_tensor("cc2out", [NCORES * 128, 96], F32).ap()
    cc3in_d = nc.dram_tensor("cc3in", [128, CCW], F32).ap()
    cc3out_d = nc.dram_tensor("cc3out", [NCORES * 128, CCW], F32).ap()
    abc_d = nc.dram_tensor("abc", [NT, 4096], F32, **dkind).ap()
    ct_d = nc.dram_tensor("ctd", [NT, 128, 1024], BF16, **dkind).ap()
    eag_d = nc.dram_tensor("eagd", [NT, 128, 32], F32, **dkind).ap()

    sb = nc.alloc_sbuf_tensor

    identf = sb("identf", [128, 128], F32).ap()
    ident = sb("ident", [128, 128], BF16).ap()
    triU = sb("triU", [128, 128], F32).ap()
    triR = sb("triR", [128, 128], F32).ap()
    mask01 = sb("mask01", [128, 128], F32).ap()
    negcol = sb("negcol", [128, 1], F32).ap()
    rmask = sb("rmask_sb", [128, 16], F32).ap()
    wup = sb("wup", [17, 512], BF16).ap()
    b_ident, b_identf, b_triU, b_triR, b_mask, b_negcol, b_rmask, b_nwb, b_whnb, b_wup = [
        Buf(n) for n in "ident identf triU triR mask01 negcol rmask nwb whnb wup".split()]

    def pool(f, r=(), w=()):
        return P.op("pool", f, reads=r, writes=w)

    def act(f, r=(), w=()):
        return P.op("act", f, reads=r, writes=w)

    def dve(f, r=(), w=()):
        return P.op("dve", f, reads=r, writes=w)

    def pe(f, r=(), w=()):
        return P.op("pe", f, reads=r, writes=w)

    pool(lambda e: e.memset(identf, 1.0), w=[b_identf])
    pool(lambda e: e.affine_select(out=identf, in_=identf, pattern=[[-1, 128]], compare_op=ALU.is_equal,
                                   fill=0.0, base=0, channel_multiplier=1), r=[b_identf], w=[b_identf])
    pool(lambda e: e.tensor_copy(out=ident, in_=identf), r=[b_identf], w=[b_ident])
    pool(lambda e: e.memset(mask01, 1.0), w=[b_mask])
    pool(lambda e: e.affine_select(out=mask01, in_=mask01, pattern=[[1, 128]], compare_op=ALU.is_ge,
                                   fill=0.0, base=0, channel_multiplier=-1), r=[b_mask], w=[b_mask])
    pool(lambda e: e.tensor_scalar(out=triU, in0=mask01, scalar1=-1.0 / 16, scalar2=None, op0=ALU.mult),
         r=[b_mask], w=[b_triU])
    pool(lambda e: e.tensor_scalar(out=triR, in0=mask01, scalar1=1.0 / 16, scalar2=-1.0 / 16, op0=ALU.mult,
                                   op1=ALU.add), r=[b_mask], w=[b_triR])
    pool(lambda e: e.memset(negcol, -1.0 / 16), w=[b_negcol])
    P.dma("sp", rmask, rmask_d, writes=[b_rmask])
    P.dma("pool", wup[0:16, :], gla_up_d, writes=[b_wup])
    P.dma("pool", wup[16:17, :], gla_gb_d, writes=[b_wup])

    WA_COLS = 4128
    WA = sb("WA", [128, 8 * WA_COLS], BF16).ap()
    WBg = sb("WBg", [128, 8, 2048], BF16).ap()
    WBo = sb("WBo", [128, 16, 1024], BF16).ap()
    b_WA, b_WBg, b_WBo = Buf("WA"), Buf("WBg"), Buf("WBo")
    W0a = WA[:, 0:8 * 3088].rearrange("p (k c) -> p k c", k=8)
    for k in range(8):
        P.dma("pool", W0a[:, k, 0:3072], gla_in_d[k * 128:(k + 1) * 128, 0:3072], writes=[b_WA], chain=False, nobar=True)
        P.dma("pool", W0a[:, k, 3072:3088], gla_in_d[k * 128:(k + 1) * 128, 5120:5136], writes=[b_WA], chain=False, nobar=True)
    for k in range(8):
        P.dma("pool", WBg[:, k, :], gla_in_d[k * 128:(k + 1) * 128, 3072:5120], writes=[b_WBg], chain=False, nobar=True)
    for k in range(16):
        P.dma("pool", WBo[:, k, :], gla_out_d[k * 128:(k + 1) * 128, :], writes=[b_WBo], chain=False, nobar=True)

    W1a = WA.rearrange("p (k c) -> p k c", k=8)

    def load_l1p1_weights():
        for k in range(8):
            P.dma("pool", W1a[:, k, 0:4096], ssd_in_d[k * 128:(k + 1) * 128, 2048:6144], writes=[b_WA], chain=False, nobar=True)
            P.dma("pool", W1a[:, k, 4096:4128], ssd_in_d[k * 128:(k + 1) * 128, 6144:6176], writes=[b_WA], chain=False, nobar=True)

    S = sb("S", [128, 2048], F32).ap()
    Sbf = sb("Sbf", [128, 2048], BF16).ap()
    Sinbf = sb("Sinbf", [128, 2048], BF16).ap()
    Bst = sb("Bst", [128, 4], F32).ap()
    dt_sb = sb("dtot", [128, 4], F32).ap()
    ones_f = sb("ones_f", [128, 128], F32).ap()
    nwT1 = sb("nwT1", [128, 8], F32).ap()
    gnT = sb("gnT", [128, 16], F32).ap()
    cw = sb("cw", [128, 32, 4], F32).ap()
    cbias = sb("cbias", [128, 32], F32).ap()
    dtb_b = sb("dtb_b", [128, 32], F32).ap()
    A_b = sb("A_b", [128, 32], F32).ap()
    D_b = sb("D_b", [128, 32], F32).ap()
    halo = sb("halo", [128, 32, 3], F32).ap()
    Ast = sb("Ast", [128, 32], F32).ap()
    dt3 = sb("dtot3", [128, 32], F32).ap()
    stg_aps = [sb(f"stg{i}", [32, 128], F32).ap() for i in range(2)]
    stg_bufs = [Buf(f"stg{i}") for i in range(2)]
    b_sc, b_halo, b_Ast = Buf("ssdconst"), Buf("halo"), Buf("Ast")

    ps = nc.alloc_psum_tensor("ps", [128, 4096], F32).ap()
    pbufs = [Buf(f"pb{i}", excl=True) for i in range(8)]
    pstate = {"i": 0}

    def bank():
        k = pstate["i"] % 8
        pstate["i"] += 1
        return ps[:, k * 512:(k + 1) * 512], pbufs[k]

    if stage >= 2:
        pool(lambda e: e.memset(ones_f, 1.0), w=[b_sc])
        P.dma("sp", dtb_b, ssd_dtb_d.broadcast_to([128, 32]), writes=[b_sc])
        P.dma("sp", D_b, ssd_d_d.broadcast_to([128, 32]), writes=[b_sc])
        P.dma("sp", A_b, ssd_alog_d.broadcast_to([128, 32]), writes=[b_sc])
        act(lambda e: e.activation(out=A_b, in_=A_b, func=AF.Exp), r=[b_sc], w=[b_sc])
        pool(lambda e: e.tensor_scalar(out=A_b, in0=A_b, scalar1=-1.0, scalar2=None, op0=ALU.mult), r=[b_sc], w=[b_sc])
        cst = {"i": 0}

        def colT(src_rows, K, dst):
            k_ = cst["i"] % 2
            cst["i"] += 1
            stg, stgb = stg_aps[k_], stg_bufs[k_]
            P.dma("sp", stg[0:K, :], src_rows, writes=[stgb])
            pcol, pcolb = bank()
            pe(lambda e: e.matmul(pcol[:, 0:K], lhsT=stg[0:K, :], rhs=identf[0:K, 0:K], start=True, stop=True),
               r=[stgb, b_identf], w=[pcolb])
            act(lambda e: e.copy(out=dst, in_=pcol[:, 0:K]), r=[pcolb], w=[b_sc])

        for k in range(4):
            colT(ssd_cw_d[k].rearrange("(cb p) -> cb p", p=128), 32, cw[:, :, k])
        colT(ssd_cb_d[0].rearrange("(cb p) -> cb p", p=128), 32, cbias)
        colT(norm_w_d[1].rearrange("(k p) -> k p", p=128), 8, nwT1)
        colT(ssd_gn_d, 16, gnT)

    arena = Arena(nc, (nc.sbuf_bytes_remaining - 256) // 64 * 64)
    NR = {}

    def setup_norm_rings(n_ht=2):
        NR["RX"] = Ring(arena, "xt", [128, D], F32, 2)
        NR["RST"] = Ring(arena, "st", [128, 16], F32, 4)
        NR["RHN"] = Ring(arena, "hn", [128, D], BF16, 1)
        NR["RHT"] = Ring(arena, "hnT", [128, 8, 128], BF16, n_ht)
        NR["junk"] = None

    def norm_tile(src_ap, nw_ap, nw_buf):
        RX, RST, RHN, RHT = NR["RX"], NR["RST"], NR["RHN"], NR["RHT"]
        xt, xb = RX.next()
        P.dma("sp", xt, src_ap, writes=[xb])
        st, stb = RST.next()
        hn, hb = RHN.next()
        junk = hn
        act(lambda e: e.activation(out=junk, in_=xt, func=AF.Square, accum_out=st[:, 0:1]), r=[xb], w=[stb, hb])
        act(lambda e: e.activation(out=st[:, 1:2], in_=st[:, 0:1], func=AF.Ln, scale=1.0 / D, bias=EPS),
            r=[stb], w=[stb])
        act(lambda e: e.activation(out=st[:, 2:3], in_=st[:, 1:2], func=AF.Exp, scale=-0.5), r=[stb], w=[stb])
        if nw_ap is None:
            act(lambda e: e.activation(out=hn, in_=xt, func=AF.Copy, scale=st[:, 2:3]), r=[xb, stb], w=[hb])
        else:
            dve(lambda e: e.scalar_tensor_tensor(out=hn, in0=xt, scalar=st[:, 2:3], in1=nw_ap, op0=ALU.mult,
                                                 op1=ALU.mult), r=[xb, stb, nw_buf], w=[hb])
        pt, ptb = bank()
        ptb16 = pt.bitcast(BF16)
        for k in range(8):
            pe(lambda e, k=k: e.transpose(out=ptb16[:, k * 128:(k + 1) * 128], in_=hn[:, k * 128:(k + 1) * 128],
                                          identity=ident), r=[hb, b_ident], w=[ptb])
        hnT, hTb = RHT.next()
        act(lambda e: e.copy(out=hnT.rearrange("p k t -> p (k t)"), in_=ptb16), r=[ptb], w=[hTb])
        return xt, xb, hnT, hTb, st, stb

    b_S = [Buf(f"S{h}") for h in range(4)]
    b_Sbf = [Buf(f"Sbf{h}") for h in range(4)]
    b_Bst = Buf("Bst")
    pool(lambda e: e.memset(S, 0.0), w=b_S)
    pool(lambda e: e.memset(Sbf, 0.0), w=b_Sbf)
    pool(lambda e: e.memset(Bst, 0.0), w=[b_Bst])

    arena.reset()
    setup_norm_rings()
    nwb = arena.alloc([128, D], F32)
    P.dma("sp", nwb, norm_w_d[0:1, :].broadcast_to([128, D]), writes=[b_nwb])
    RGK = Ring(arena, "gk", [32, 128], BF16, 2)
    for ap_, b_ in zip(RGK.aps, RGK.bufs):
        pool(lambda e, ap_=ap_: e.memset(ap_, 1.0), w=[b_])
    RLA = Ring(arena, "la", [128, 512], F32, 1)
    REB = Ring(arena, "eb", [128, 3, 512], F32, 1)
    RSM = Ring(arena, "sm", [128, 16], F32, 2)
    RQK = Ring(arena, "qk", [128, 3, 512], BF16, 1)
    RQKT = Ring(arena, "qkT", [128, 8, 128], BF16, 1)
    RQG = Ring(arena, "qgt", [128, 4, 128], BF16, 2)
    RV = Ring(arena, "v", [128, 2048], BF16, 1)
    RAM = Ring(arena, "am", [128, 4, 128], BF16, 1)
    RO = Ring(arena, "o", [128, 512], F32, 4)
    b_oloc = [[Buf(f"oloc{i}_{h}") for h in range(4)] for i in range(NT)]
    b_qgd = [Buf(f"qgd{i}") for i in range(NT)]
    LNSCALE = float(np.log(128.0 ** -0.5))

    nt_next = norm_tile(x_d[0:128, :], nwb, b_nwb)
    for i in range(NT):
        xt, xb, hnT, hTb, _st, _stb = nt_next
        if i + 1 < NT:
            nt_next = norm_tile(x_d[(i + 1) * 128:(i + 2) * 128, :], nwb, b_nwb)
        pg, pgb = bank()
        for k in range(8):
            pe(lambda e, k=k, pg=pg, hnT=hnT: e.matmul(pg[0:16, 0:128], lhsT=W0a[:, k, 3072:3088], rhs=hnT[:, k, :],
                                                       start=(k == 0), stop=(k == 7)), r=[hTb, b_WA], w=[pgb])
        gk, gkb = RGK.next()
        act(lambda e, gk=gk, pg=pg: e.copy(out=gk[0:16, :], in_=pg[0:16, 0:128]), r=[pgb], w=[gkb])
        pz, pzb = bank()
        pe(lambda e, pz=pz, gk=gk: e.matmul(pz, lhsT=gk[0:17, :], rhs=wup[0:17, :], start=True, stop=True),
           r=[gkb, b_wup], w=[pzb])
        la, lab = RLA.next()
        act(lambda e, la=la, pz=pz: e.activation(out=la, in_=pz, func=AF.Exp, scale=-1.0), r=[pzb], w=[lab])
        act(lambda e, la=la: e.activation(out=la, in_=la, func=AF.Ln, bias=1.0), r=[lab], w=[lab])
        v, vb = RV.next()

        def v_block(c, hnT=hnT, hTb=hTb, v=v, vb=vb):
            pv, pvb = bank()
            for k in range(8):
                pe(lambda e, k=k, c=c, pv=pv, hnT=hnT: e.matmul(pv, lhsT=hnT[:, k, :],
                                                             rhs=W0a[:, k, 1024 + c * 512:1024 + (c + 1) * 512],
                                                             start=(k == 0), stop=(k == 7)), r=[hTb, b_WA], w=[pvb])
            act(lambda e, c=c, pv=pv, v=v: e.copy(out=v[:, c * 512:(c + 1) * 512], in_=pv), r=[pvb], w=[vb])

        v_block(0)
        v_block(1)
        pb_, pbb = bank()
        pr, prb = bank()
        pl, plb = bank()
        pe(lambda e, pb_=pb_, la=la: e.matmul(pb_, lhsT=triU, rhs=la, start=True, stop=True), r=[b_triU, lab], w=[pbb])
        pe(lambda e, pr=pr, la=la: e.matmul(pr, lhsT=triR, rhs=la, start=True, stop=True), r=[b_triR, lab], w=[prb])
        for h in range(4):
            pe(lambda e, h=h, pl=pl, la=la: e.matmul(pl[:, h:h + 1], lhsT=la[:, h * 128:(h + 1) * 128], rhs=negcol,
                                                     start=True, stop=True), r=[lab, b_negcol], w=[plb])
        eb, ebb = REB.next()
        act(lambda e, eb=eb, pb_=pb_: e.activation(out=eb[:, 0, :], in_=pb_, func=AF.Exp, bias=LNSCALE), r=[pbb], w=[ebb])
        act(lambda e, eb=eb, pb_=pb_: e.activation(out=eb[:, 1, :], in_=pb_, func=AF.Exp, scale=-1.0), r=[pbb], w=[ebb])
        act(lambda e, eb=eb, pr=pr: e.activation(out=eb[:, 2, :], in_=pr, func=AF.Exp), r=[prb], w=[ebb])
        sm, smb = RSM.next()
        act(lambda e, sm=sm, pl=pl: e.activation(out=sm[:, 0:4], in_=pl[:, 0:4], func=AF.Exp), r=[plb], w=[smb])
        act(lambda e, sm=sm: e.activation(out=sm[:, 4:8], in_=Bst, func=AF.Exp), r=[b_Bst], w=[smb])
        dve(lambda e, pl=pl: e.tensor_tensor(out=Bst, in0=Bst, in1=pl[:, 0:4], op=ALU.add), r=[b_Bst, plb], w=[b_Bst])
        pq, pqb = bank()
        pk, pkb = bank()
        for k in range(8):
            pe(lambda e, k=k, pq=pq, hnT=hnT: e.matmul(pq, lhsT=hnT[:, k, :], rhs=W0a[:, k, 0:512], start=(k == 0),
                                                       stop=(k == 7)), r=[hTb, b_WA], w=[pqb])
        for k in range(8):
            pe(lambda e, k=k, pk=pk, hnT=hnT: e.matmul(pk, lhsT=hnT[:, k, :], rhs=W0a[:, k, 512:1024], start=(k == 0),
                                                       stop=(k == 7)), r=[hTb, b_WA], w=[pkb])
        qk, qkb = RQK.next()
        dve(lambda e, qk=qk, pq=pq, eb=eb: e.tensor_tensor(out=qk[:, 0, :], in0=pq, in1=eb[:, 0, :], op=ALU.mult),
            r=[pqb, ebb], w=[qkb])
        dve(lambda e, qk=qk, pk=pk, eb=eb: e.tensor_tensor(out=qk[:, 1, :], in0=pk, in1=eb[:, 1, :], op=ALU.mult),
            r=[pkb, ebb], w=[qkb])
        dve(lambda e, qk=qk, pk=pk, eb=eb: e.tensor_tensor(out=qk[:, 2, :], in0=pk, in1=eb[:, 2, :], op=ALU.mult),
            r=[pkb, ebb], w=[qkb])
        v_block(2)
        v_block(3)
        pt2, pt2b = bank()
        pt2b16 = pt2.bitcast(BF16)
        for j in range(8):
            pe(lambda e, j=j, qk=qk, pt2b16=pt2b16: e.transpose(out=pt2b16[:, j * 128:(j + 1) * 128],
                                                                in_=qk[:, j // 4, (j % 4) * 128:(j % 4 + 1) * 128],
                                                                identity=ident), r=[qkb, b_ident], w=[pt2b])
        qkT, qkTb = RQKT.next()
        act(lambda e, qkT=qkT, pt2b16=pt2b16: e.copy(out=qkT.rearrange("p k t -> p (k t)"), in_=pt2b16),
            r=[pt2b], w=[qkTb])
        qg, qgb = RQG.next()
        dve(lambda e, qg=qg, qkT=qkT, sm=sm: e.tensor_tensor(
            out=qg, in0=qkT[:, 0:4, :], in1=sm[:, 4:8].unsqueeze(2).broadcast_to([128, 4, 128]), op=ALU.mult),
            r=[qkTb, smb], w=[qgb])
        P.dma("sp", qg_d[i], qg.rearrange("p h t -> p (h t)"), reads=[qgb], writes=[b_qgd[i]])
        pa, pab = bank()
        for h in range(4):
            pe(lambda e, h=h, pa=pa, qkT=qkT: e.matmul(pa[:, h * 128:(h + 1) * 128], lhsT=qkT[:, 4 + h, :],
                                                       rhs=qkT[:, h, :], start=True, stop=True), r=[qkTb], w=[pab])
        am, amb = RAM.next()
        dve(lambda e, am=am, pa=pa: e.tensor_tensor(out=am, in0=pa.rearrange("p (h t) -> p h t", h=4),
                                                    in1=mask01.unsqueeze(1).broadcast_to([128, 4, 128]), op=ALU.mult),
            r=[pab, b_mask], w=[amb])
        for h in range(4):
            po, pob = bank()
            hs = slice(h * 512, (h + 1) * 512)
            pe(lambda e, h=h, po=po, qkT=qkT, hs=hs: e.matmul(po, lhsT=qkT[:, h, :], rhs=Sbf[:, hs], start=True, stop=False),
               r=[qkTb, b_Sbf[h]], w=[pob])
            pe(lambda e, h=h, po=po, am=am, v=v, hs=hs: e.matmul(po, lhsT=am[:, h, :], rhs=v[:, hs], start=False, stop=True),
               r=[amb, vb], w=[pob])
            o_sb, ob = RO.next()
            act(lambda e, o_sb=o_sb, po=po: e.copy(out=o_sb, in_=po), r=[pob], w=[ob])
            P.dma("sp", oloc_d[i][:, hs], o_sb, reads=[ob], writes=[b_oloc[i][h]], sembuf=ob)
            pu, pub = bank()
            pe(lambda e, h=h, pu=pu, qk=qk, v=v, hs=hs: e.matmul(pu, lhsT=qk[:, 2, h * 128:(h + 1) * 128], rhs=v[:, hs],
                                                                start=True, stop=True), r=[qkb, vb], w=[pub])
            dve(lambda e, h=h, pu=pu, sm=sm, hs=hs: e.scalar_tensor_tensor(out=S[:, hs], in0=S[:, hs], scalar=sm[:, h:h + 1],
                                                                         in1=pu, op0=ALU.mult, op1=ALU.add),
                r=[b_S[h], smb, pub], w=[b_S[h]])
            pool(lambda e, hs=hs: e.tensor_copy(out=Sbf[:, hs], in_=S[:, hs]), r=[b_S[h]], w=[b_Sbf[h]])

    b_ccin, b_ccout, b_cc = Buf("ccin"), Buf("ccout"), Buf("cc")
    b_dt = Buf("dtot")
    act(lambda e: e.activation(out=dt_sb, in_=Bst, func=AF.Exp), r=[b_Bst], w=[b_dt])
    P.dma("sp", ccin_d[:, 0:2048], S, reads=b_S, writes=[b_ccin], sembuf=b_ccin)
    P.dma("sp", ccin_d[:, 2048:2052], dt_sb, reads=[b_dt], writes=[b_ccin], sembuf=b_ccin)
    P.custom_async("pool", lambda e: e.collective_compute("AllGather", ALU.bypass, replica_groups=[list(range(NCORES))],
                                                          ins=[ccin_d], outs=[ccout_d]), b_cc, 1,
                   reads=[b_ccin], writes=[b_ccout])
    P.barrier()
    arena.reset()
    Sin = arena.alloc([128, 2048], F32)
    b_Sin, b_Sinbf = Buf("Sin"), Buf("Sinbf")
    pool(lambda e: e.memset(Sin, 0.0), w=[b_Sin])
    RG = Ring(arena, "G", [128, CCW], F32, 2)
    RDP = Ring(arena, "dp", [128, 4], F32, 2)
    for j in range(NCORES - 1):
        G, Gb = RG.next()
        P.dma("sp", G, ccout_d[j * 128:(j + 1) * 128, :], reads=[b_ccout], writes=[Gb])
        dp, dpb = RDP.next()
        dve(lambda e, dp=dp, G=G, j=j: e.tensor_scalar(out=dp, in0=G[:, 2048:2052], scalar1=-1.0, scalar2=rmask[:, j:j + 1],
                                                      op0=ALU.add, op1=ALU.mult), r=[Gb, b_rmask], w=[dpb])
        dve(lambda e, dp=dp: e.tensor_scalar(out=dp, in0=dp, scalar1=1.0, scalar2=None, op0=ALU.add), r=[dpb], w=[dpb])
        pool(lambda e, G=G, j=j: e.tensor_scalar(out=G[:, 0:2048], in0=G[:, 0:2048], scalar1=rmask[:, j:j + 1], scalar2=None,
                                                op0=ALU.mult), r=[Gb, b_rmask], w=[Gb])
        for h in range(4):
            hs = slice(h * 512, (h + 1) * 512)
            dve(lambda e, hs=hs, h=h, dp=dp, G=G: e.scalar_tensor_tensor(out=Sin[:, hs], in0=Sin[:, hs], scalar=dp[:, h:h + 1],
                                                                        in1=G[:, hs], op0=ALU.mult, op1=ALU.add),
                r=[b_Sin, dpb, Gb], w=[b_Sin])
    act(lambda e: e.copy(out=Sinbf, in_=Sin), r=[b_Sin], w=[b_Sinbf])

    P.barrier()
    arena.reset()
    setup_norm_rings()
    nwb = arena.alloc([128, D], F32)
    whnb = arena.alloc([128, 512], F32)
    P.dma("sp", nwb, norm_w_d[0:1, :].broadcast_to([128, D]), writes=[b_nwb])
    P.dma("sp", whnb, gla_hn_d.broadcast_to([128, 512]), writes=[b_whnb])
    if stage >= 2 and 'noload' not in KOPT:
        load_l1p1_weights()
    ROL = Ring(arena, "ol", [128, 512], F32, 4)
    RQ2 = Ring(arena, "qg2", [128, 4, 128], BF16, 2)
    RSG = Ring(arena, "sg", [128, 512], F32, 2)
    RO2 = Ring(arena, "o2", [128, 512], F32, 2)
    ROG = Ring(arena, "og", [128, 2048], BF16, 1)
    ROGT = Ring(arena, "ogT", [128, 16, 128], BF16, 1)
    RX1 = Ring(arena, "x1t", [128, D], F32, 1)
    RS2 = Ring(arena, "st2", [128, 16], F32, 2)
    b_x1d = [Buf(f"x1d{i}") for i in range(NT)]
    junk2 = arena.alloc([128, 512], BF16)
    x1_dst = out_d if stage == 1 else x1_d

    nt_next = norm_tile(x_d[0:128, :], nwb, b_nwb)
    for i in range(NT):
        xt, xb, hnT, hTb, _st, _stb = nt_next
        if i + 1 < NT:
            nt_next = norm_tile(x_d[(i + 1) * 128:(i + 2) * 128, :], nwb, b_nwb)
        qg2, q2b = RQ2.next()
        P.dma("sp", qg2.rearrange("p h t -> p (h t)"), qg_d[i], reads=[b_qgd[i]], writes=[q2b])
        og, ogb = ROG.next()
        st2, s2b = RS2.next()
        for h in range(4):
            hs = slice(h * 512, (h + 1) * 512)
            ol, olb = ROL.next()
            P.dma("sp", ol, oloc_d[i][:, hs], reads=[b_oloc[i][h]], writes=[olb])
            pgt, pgtb = bank()
            for k in range(8):
                pe(lambda e, k=k, pgt=pgt, hnT=hnT, hs=hs: e.matmul(pgt, lhsT=hnT[:, k, :], rhs=WBg[:, k, hs], start=(k == 0),
                                                                   stop=(k == 7)), r=[hTb, b_WBg], w=[pgtb])
            sg, sgb = RSG.next()
            act(lambda e, sg=sg, pgt=pgt: e.activation(out=sg, in_=pgt, func=AF.Silu), r=[pgtb], w=[sgb])
            pc, pcb = bank()
            pe(lambda e, h=h, pc=pc, qg2=qg2, hs=hs: e.matmul(pc, lhsT=qg2[:, h, :], rhs=Sinbf[:, hs], start=True, stop=True),
               r=[q2b, b_Sinbf], w=[pcb])
            o2, o2b = RO2.next()
            dve(lambda e, o2=o2, pc=pc, ol=ol: e.tensor_tensor(out=o2, in0=pc, in1=ol, op=ALU.add), r=[pcb, olb], w=[o2b])
            act(lambda e, o2=o2, st2=st2, h=h: e.activation(out=junk2, in_=o2, func=AF.Square, accum_out=st2[:, h:h + 1]),
                r=[o2b], w=[s2b])
            act(lambda e, st2=st2, h=h: e.activation(out=st2[:, 4 + h:5 + h], in_=st2[:, h:h + 1], func=AF.Ln,
                                                    scale=1.0 / 512, bias=EPS), r=[s2b], w=[s2b])
            act(lambda e, st2=st2, h=h: e.activation(out=st2[:, 8 + h:9 + h], in_=st2[:, 4 + h:5 + h], func=AF.Exp,
                                                    scale=-0.5), r=[s2b], w=[s2b])
            dve(lambda e, o2=o2, st2=st2, sg=sg, h=h: e.scalar_tensor_tensor(out=o2, in0=o2, scalar=st2[:, 8 + h:9 + h], in1=sg,
                                                                            op0=ALU.mult, op1=ALU.mult),
                r=[o2b, s2b, sgb], w=[o2b])
            pool(lambda e, og=og, o2=o2, hs=hs: e.tensor_tensor(out=og[:, hs], in0=o2, in1=whnb, op=ALU.mult),
                 r=[o2b, b_whnb], w=[ogb])
        ogT, ogTb = ROGT.next()
        for half in range(2):
            pt3, pt3b = bank()
            pt3b16 = pt3.bitcast(BF16)
            for j in range(8):
                kk = half * 8 + j
                pe(lambda e, j=j, kk=kk, og=og, pt3b16=pt3b16: e.transpose(out=pt3b16[:, j * 128:(j + 1) * 128],
                                                                          in_=og[:, kk * 128:(kk + 1) * 128],
                                                                          identity=ident), r=[ogb, b_ident], w=[pt3b])
            act(lambda e, half=half, ogT=ogT, pt3b16=pt3b16: e.copy(
                out=ogT[:, half * 8:(half + 1) * 8, :].rearrange("p k t -> p (k t)"), in_=pt3b16), r=[pt3b], w=[ogTb])
        x1t, x1b = RX1.next()
        for c in range(2):
            cs = slice(c * 512, (c + 1) * 512)
            py, pyb = bank()
            for k in range(16):
                pe(lambda e, k=k, py=py, ogT=ogT, cs=cs: e.matmul(py, lhsT=ogT[:, k, :], rhs=WBo[:, k, cs], start=(k == 0),
                                                                 stop=(k == 15)), r=[ogTb, b_WBo], w=[pyb])
            dve(lambda e, x1t=x1t, py=py, xt=xt, cs=cs: e.tensor_tensor(out=x1t[:, cs], in0=py, in1=xt[:, cs], op=ALU.add),
                r=[pyb, xb], w=[x1b])
        P.dma("sp", x1_dst[i * 128:(i + 1) * 128, :], x1t, reads=[x1b], writes=[b_x1d[i]], sembuf=x1b)
        if stage >= 2 and 6 <= i < 14 and 'noscale' not in KOPT:
            kz = i - 6
            pool(lambda e, kz=kz: e.tensor_scalar(out=W1a[:, kz, :], in0=W1a[:, kz, :], scalar1=nwT1[:, kz:kz + 1], scalar2=None,
                                                 op0=ALU.mult), r=[b_WA, b_sc], w=[b_WA])

    def emit_l1():
        P.barrier()
        arena.reset()
        setup_norm_rings(n_ht=2)
        H, Hbf, Hinbf = S, Sbf, Sinbf
        b_H = [Buf(f"H{g}") for g in range(8)]
        b_Hbf = [Buf(f"Hbf{g}") for g in range(8)]
        b_Hinbf = Buf("Hinbf")
        pool(lambda e: e.memset(H, 0.0), r=b_S, w=b_H)
        pool(lambda e: e.memset(Hbf, 0.0), r=b_Sbf, w=b_Hbf)
        pool(lambda e: e.memset(Ast, 0.0), w=[b_Ast])
        for k in range(8 if 'nol1w' not in KOPT else 0):
            P.dma("pool", WBg[:, k, :], ssd_in_d[k * 128:(k + 1) * 128, 0:2048], writes=[b_WBg], chain=False, nobar=True)
        for k in range(16 if 'nol1w' not in KOPT else 0):
            P.dma("pool", WBo[:, k, :], ssd_out_d[k * 128:(k + 1) * 128, :], writes=[b_WBo], chain=False, nobar=True)

        if DBG == 11:
            return
        xt, xb, hnT, hTb, _st, _stb = norm_tile(x1_d[(NT - 1) * 128:NT * 128, :], None, None)
        ph, phb = bank()
        for cb in range(32):
            for k in range(8):
                pe(lambda e, cb=cb, k=k, ph=ph, hnT=hnT: e.matmul(ph[:, cb * 4:(cb + 1) * 4], lhsT=W1a[:, k, cb * 128:(cb + 1) * 128],
                                                                 rhs=hnT[:, k, 124:128], start=(k == 0), stop=(k == 7)),
                   r=[hTb, b_WA], w=[phb])
        hl = arena.alloc([128, 96], F32)
        b_hl = Buf("hl")
        act(lambda e: e.copy(out=hl.rearrange("p (c j) -> p c j", j=3), in_=ph[:, 0:128].rearrange("p (c j) -> p c j", j=4)[:, :, 1:4]),
            r=[phb], w=[b_hl])
        if DBG == 12:
            return
        b_cc2in, b_cc2out, b_cc2 = Buf("cc2in"), Buf("cc2out"), Buf("cc2")
        P.dma("sp", cc2in_d, hl, reads=[b_hl], writes=[b_cc2in])
        P.custom_async("pool", lambda e: e.collective_compute("AllGather", ALU.bypass, replica_groups=[list(range(NCORES))],
                                                              ins=[cc2in_d], outs=[cc2out_d]), b_cc2, 1,
                       reads=[b_cc2in], writes=[b_cc2out])
        RXTK = Ring(arena, "xtok", [128, 2048], BF16, 1)
        hg = RXTK.aps[0].bitcast(F32)[:, 0:768].rearrange("p (r c) -> p r c", r=8)
        b_hg = Buf("hg")
        P.dma("sp", hg, cc2out_d.rearrange("(r p) c -> p r c", p=128), reads=[b_cc2out], writes=[b_hg])
        pool(lambda e: e.memset(halo, 0.0), w=[b_halo])
        halo_f = halo.rearrange("p c j -> p (c j)")
        for j in range(NCORES - 1):
            dve(lambda e, j=j: e.scalar_tensor_tensor(out=halo_f, in0=hg[:, j, :], scalar=rmask[:, 8 + j:9 + j], in1=halo_f,
                                                      op0=ALU.mult, op1=ALU.add), r=[b_hg, b_rmask, b_halo], w=[b_halo])

        if DBG == 1:
            return
        RRAW = Ring(arena, "raw", [128, 8, 131], F32, 1)
        RACC = Ring(arena, "acc", [128, 8, 128], F32, 1)
        acc_bufs = [Buf(f"acc{c}") for c in range(8)]
        RXBC = Ring(arena, "xbcT", [128, 32, 128], BF16, 1)
        RDT = Ring(arena, "dts", [128, 8, 32], F32, 2)
        RAT = Ring(arena, "aT", [32, 128], F32, 1)
        RAB = Ring(arena, "ab", [128, 4, 128], F32, 2)
        RE = Ring(arena, "E", [128, 4, 128], F32, 1)
        RGm = Ring(arena, "Gm", [128, 4, 128], BF16, 1)
        RCBM = Ring(arena, "cbm", [128, 4, 128], F32, 1)
        RTMP = Ring(arena, "ctmp", [128, 128], F32, 2)
        RXG = Ring(arena, "xg", [128, 3, 256], BF16, 2)
        RBT = Ring(arena, "Btok", [128, 1024], BF16, 1)
        RY = Ring(arena, "y", [128, 256], F32, 2)
        b_yloc = [[Buf(f"yloc{i}_{g}") for g in range(8)] for i in range(NT)]
        b_ctd = [Buf(f"ctd{i}") for i in range(NT)]
        b_eagd = [Buf(f"eagd{i}") for i in range(NT)]
        b_abcd = [Buf(f"abcd{i}") for i in range(NT)]

        nt_next = norm_tile(x1_d[0:128, :], None, None)
        for i in range(NT):
            xt, xb, hnT, hTb, _st, _stb = nt_next
            if i + 1 < NT:
                nt_next = norm_tile(x1_d[(i + 1) * 128:(i + 2) * 128, :], None, None)
            dts, dtsb = RDT.next()
            pdt, pdtb = bank()
            for k in range(8):
                pe(lambda e, k=k, pdt=pdt, hnT=hnT: e.matmul(pdt[:, 0:32], lhsT=hnT[:, k, :], rhs=W1a[:, k, 4096:4128],
                                                            start=(k == 0), stop=(k == 7)), r=[hTb, b_WA], w=[pdtb])
            dve(lambda e, dts=dts, pdt=pdt: e.tensor_tensor(out=dts[:, 0, :], in0=pdt[:, 0:32], in1=dtb_b, op=ALU.add),
                r=[pdtb, b_sc], w=[dtsb])
            act(lambda e, dts=dts: e.activation(out=dts[:, 0, :], in_=dts[:, 0, :], func=AF.Exp), r=[dtsb], w=[dtsb])
            act(lambda e, dts=dts: e.activation(out=dts[:, 1, :], in_=dts[:, 0, :], func=AF.Ln, bias=1.0), r=[dtsb], w=[dtsb])
            dve(lambda e, dts=dts: e.tensor_tensor(out=dts[:, 2, :], in0=dts[:, 1, :], in1=A_b, op=ALU.mult),
                r=[dtsb, b_sc], w=[dtsb])
            xbcT, xbcb = RXBC.next()

            def conv_group(gq, hnT=hnT, hTb=hTb, xbcT=xbcT, xbcb=xbcb):
                raw, rawb = RRAW.next()
                acc, _accb = RACC.next()
                accbs = acc_bufs
                pool(lambda e, raw=raw: e.tensor_copy(out=raw[:, :, 0:3], in_=halo[:, gq * 8:(gq + 1) * 8, :]),
                     r=[b_halo], w=[rawb])
                if DBG == 210:
                    return
                for half in range(2):
                    pxb, pxbb = bank()
                    for q in range(4):
                        cb = gq * 8 + half * 4 + q
                        for k in range(8):
                            pe(lambda e, cb=cb, q=q, k=k, pxb=pxb: e.matmul(pxb[:, q * 128:(q + 1) * 128],
                                                                           lhsT=W1a[:, k, cb * 128:(cb + 1) * 128], rhs=hnT[:, k, :],
                                                                           start=(k == 0), stop=(k == 7)),
                               r=[hTb, b_WA], w=[pxbb])
                    act(lambda e, pxb=pxb, raw=raw, half=half: e.copy(out=raw[:, half * 4:(half + 1) * 4, 3:131],
                                                                      in_=pxb.rearrange("p (q t) -> p q t", q=4)),
                        r=[pxbb], w=[rawb])
                    if DBG == 211:
                        continue
                    for q in range(4):
                        cl = half * 4 + q
                        cb = gq * 8 + cl
                        dve(lambda e, cb=cb, cl=cl, q=q, pxb=pxb, acc=acc: e.scalar_tensor_tensor(
                            out=acc[:, cl, :], in0=pxb[:, q * 128:(q + 1) * 128], scalar=cw[:, cb, 3:4],
                            in1=cbias[:, cb:cb + 1].broadcast_to([128, 128]), op0=ALU.mult, op1=ALU.add),
                            r=[pxbb, b_sc], w=[accbs[cl]])
                if DBG in (211, 212):
                    return
                for cl in range(8):
                    cb = gq * 8 + cl
                    for k in range(3):
                        if cl % 4 != 3:
                            dve(lambda e, cb=cb, cl=cl, k=k, raw=raw, acc=acc: e.scalar_tensor_tensor(
                                out=acc[:, cl, :], in0=raw[:, cl, k:k + 128], scalar=cw[:, cb, k:k + 1], in1=acc[:, cl, :],
                                op0=ALU.mult, op1=ALU.add), r=[rawb, b_sc, accbs[cl]], w=[accbs[cl]])
                        else:
                            tmp, tmpb = RTMP.next()
                            pool(lambda e, cb=cb, cl=cl, k=k, raw=raw, tmp=tmp: e.tensor_scalar(
                                out=tmp, in0=raw[:, cl, k:k + 128], scalar1=cw[:, cb, k:k + 1], scalar2=None, op0=ALU.mult),
                                r=[rawb, b_sc], w=[tmpb])
                            pool(lambda e, cl=cl, acc=acc, tmp=tmp: e.tensor_tensor(out=acc[:, cl, :], in0=acc[:, cl, :], in1=tmp,
                                                                                  op=ALU.add), r=[tmpb, accbs[cl]], w=[accbs[cl]])
                if DBG == 213:
                    return
                pool(lambda e, raw=raw: e.tensor_copy(out=halo[:, gq * 8:(gq + 1) * 8, :], in_=raw[:, :, 128:131]),
                     r=[rawb], w=[b_halo])
                if DBG == 214:
                    return
                act(lambda e, acc=acc, xbcT=xbcT: e.activation(out=xbcT[:, gq * 8:(gq + 1) * 8, :], in_=acc, func=AF.Silu),
                    r=accbs, w=[xbcb])

            if DBG == 20:
                return
            conv_group(0)
            if DBG == 21:
                return
            pa_, pab_ = bank()
            pe(lambda e, pa_=pa_, dts=dts: e.matmul(pa_[:, 0:32], lhsT=mask01, rhs=dts[:, 2, :], start=True, stop=True),
               r=[b_mask, dtsb], w=[pab_])
            pe(lambda e, pa_=pa_, dts=dts: e.matmul(pa_[0:32, 128:256], lhsT=dts[:, 2, :], rhs=mask01, start=True, stop=True),
               r=[b_mask, dtsb], w=[pab_])
            pe(lambda e, pa_=pa_, dts=dts: e.matmul(pa_[:, 256:288], lhsT=ones_f, rhs=dts[:, 2, :], start=True, stop=True),
               r=[b_sc, dtsb], w=[pab_])
            aT, aTb = RAT.next()
            act(lambda e, aT=aT, pa_=pa_: e.copy(out=aT, in_=pa_[0:32, 128:256]), r=[pab_], w=[aTb])
            P.dma("sp", abc_d[i].rearrange("(h l) -> h l", h=32), aT, reads=[aTb], writes=[b_abcd[i]])
            act(lambda e, dts=dts, pa_=pa_: e.copy(out=dts[:, 3, :], in_=pa_[:, 0:32]), r=[pab_], w=[dtsb])
            act(lambda e, dts=dts, pa_=pa_: e.activation(out=dts[:, 4, :], in_=pa_[:, 0:32], func=AF.Exp), r=[pab_], w=[dtsb])
            act(lambda e, dts=dts, pa_=pa_: e.activation(out=dts[:, 6, :], in_=pa_[:, 256:288], func=AF.Exp), r=[pab_], w=[dtsb])
            dve(lambda e, dts=dts, pa_=pa_: e.tensor_tensor(out=dts[:, 5, :], in0=pa_[:, 256:288], in1=dts[:, 3, :], op=ALU.subtract),
                r=[pab_, dtsb], w=[dtsb])
            act(lambda e, dts=dts: e.activation(out=dts[:, 5, :], in_=dts[:, 5, :], func=AF.Exp), r=[dtsb], w=[dtsb])
            dve(lambda e, dts=dts: e.tensor_tensor(out=dts[:, 5, :], in0=dts[:, 5, :], in1=dts[:, 1, :], op=ALU.mult),
                r=[dtsb], w=[dtsb])
            dve(lambda e, dts=dts: e.tensor_tensor(out=dts[:, 7, :], in0=dts[:, 3, :], in1=Ast, op=ALU.add),
                r=[dtsb, b_Ast], w=[dtsb])
            act(lambda e, dts=dts: e.activation(out=dts[:, 7, :], in_=dts[:, 7, :], func=AF.Exp), r=[dtsb], w=[dtsb])
            P.dma("sp", eag_d[i], dts[:, 7, :], reads=[dtsb], writes=[b_eagd[i]], sembuf=dtsb)
            dve(lambda e, pa_=pa_: e.tensor_tensor(out=Ast, in0=Ast, in1=pa_[:, 256:288], op=ALU.add), r=[b_Ast, pab_], w=[b_Ast])
            if DBG == 22:
                return
            conv_group(1)
            conv_group(2)
            conv_group(3)
            P.dma("sp", ct_d[i].rearrange("p (g t) -> p g t", g=8), xbcT[:, 24:32, :], reads=[xbcb], writes=[b_ctd[i]], sembuf=xbcb)
            xtok, xtokb = RXTK.next()
            for half in range(2):
                ptx, ptxb = bank()
                ptx16 = ptx.bitcast(BF16)
                for j in range(8):
                    cb = half * 8 + j
                    pe(lambda e, j=j, cb=cb, ptx16=ptx16, xbcT=xbcT: e.transpose(out=ptx16[:, j * 128:(j + 1) * 128], in_=xbcT[:, cb, :],
                                                                                identity=ident), r=[xbcb, b_ident], w=[ptxb])
                act(lambda e, half=half, xtok=xtok, ptx16=ptx16: e.copy(out=xtok[:, half * 1024:(half + 1) * 1024], in_=ptx16),
                    r=[ptxb], w=[xtokb])
            Btok, Btokb = RBT.next()
            ptb_, ptbb = bank()
            ptb16_ = ptb_.bitcast(BF16)
            for j in range(8):
                pe(lambda e, j=j, ptb16_=ptb16_, xbcT=xbcT: e.transpose(out=ptb16_[:, j * 128:(j + 1) * 128], in_=xbcT[:, 16 + j, :],
                                                                       identity=ident), r=[xbcb, b_ident], w=[ptbb])
            act(lambda e, Btok=Btok, ptb16_=ptb16_: e.copy(out=Btok, in_=ptb16_), r=[ptbb], w=[Btokb])
            if DBG == 23:
                return
            cbm = None
            for g in range(8):
                if g % 4 == 0:
                    pcb, pcbb = bank()
                    for gg in range(4):
                        pe(lambda e, gg=gg, g=g, pcb=pcb, xbcT=xbcT: e.matmul(pcb[:, gg * 128:(gg + 1) * 128], lhsT=xbcT[:, 16 + g + gg, :],
                                                                            rhs=xbcT[:, 24 + g + gg, :], start=True, stop=True),
                           r=[xbcb], w=[pcbb])
                    cbm, cbmb = RCBM.next()
                    dve(lambda e, cbm=cbm, pcb=pcb: e.tensor_tensor(out=cbm, in0=pcb.rearrange("p (g t) -> p g t", g=4),
                                                                    in1=mask01.unsqueeze(1).broadcast_to([128, 4, 128]), op=ALU.mult),
                        r=[pcbb, b_mask], w=[cbmb])
                ab, abb = RAB.next()
                P.dma("sp", ab.rearrange("p h l -> p (h l)"), abc_d[i][g * 512:(g + 1) * 512].unsqueeze(0).broadcast_to([128, 512]),
                      reads=[b_abcd[i]], writes=[abb])
                E, Eb = RE.next()
                for h in range(4):
                    eng = dve if h % 2 == 0 else pool
                    eng(lambda e, h=h, g=g, E=E, ab=ab, dts=dts: e.tensor_scalar(out=E[:, h, :], in0=ab[:, h, :],
                                                                               scalar1=dts[:, 3, 4 * g + h:4 * g + h + 1], scalar2=0.0,
                                                                               op0=ALU.subtract, op1=ALU.min), r=[abb, dtsb], w=[Eb])
                act(lambda e, E=E: e.activation(out=E, in_=E, func=AF.Exp), r=[Eb], w=[Eb])
                Gm, Gmb = RGm.next()
                pool(lambda e, Gm=Gm, E=E, cbm=cbm, g=g: e.tensor_tensor(out=Gm, in0=E,
                                                                         in1=cbm[:, g % 4, :].unsqueeze(1).broadcast_to([128, 4, 128]),
                                                                         op=ALU.mult), r=[Eb, cbmb], w=[Gmb])
                xg, xgb = RXG.next()
                gs = slice(g * 256, (g + 1) * 256)
                hsl = slice(4 * g, 4 * g + 4)
                xv = xtok[:, gs].rearrange("p (h c) -> p h c", h=4)
                dve(lambda e, xg=xg, xv=xv, dts=dts, hsl=hsl: e.tensor_tensor(out=xg[:, 0, :].rearrange("p (h c) -> p h c", h=4), in0=xv,
                                                                             in1=dts[:, 1, hsl].unsqueeze(2).broadcast_to([128, 4, 64]),
                                                                             op=ALU.mult), r=[xtokb, dtsb], w=[xgb])
                pool(lambda e, xg=xg, xv=xv, dts=dts, hsl=hsl: e.tensor_tensor(out=xg[:, 1, :].rearrange("p (h c) -> p h c", h=4), in0=xv,
                                                                              in1=dts[:, 5, hsl].unsqueeze(2).broadcast_to([128, 4, 64]),
                                                                              op=ALU.mult), r=[xtokb, dtsb], w=[xgb])
                pool(lambda e, xg=xg, xv=xv, hsl=hsl: e.tensor_tensor(out=xg[:, 2, :].rearrange("p (h c) -> p h c", h=4), in0=xv,
                                                                     in1=D_b[:, hsl].unsqueeze(2).broadcast_to([128, 4, 64]),
                                                                     op=ALU.mult), r=[xtokb, b_sc], w=[xgb])
                pyo, pyob = bank()
                py_g = pyo[:, 0:256]
                po_g = pyo[:, 256:512]
                pe(lambda e, py_g=py_g, xg=xg: e.matmul(py_g, lhsT=ident, rhs=xg[:, 2, :], start=True, stop=False),
                   r=[b_ident, xgb], w=[pyob])
                for h in range(4):
                    pe(lambda e, h=h, py_g=py_g, Gm=Gm, xg=xg: e.matmul(py_g[:, h * 64:(h + 1) * 64], lhsT=Gm[:, h, :],
                                                                       rhs=xg[:, 0, h * 64:(h + 1) * 64], start=False, stop=(h == 3)),
                       r=[Gmb, xgb], w=[pyob])
                pe(lambda e, po_g=po_g, g=g, gs=gs, xbcT=xbcT: e.matmul(po_g, lhsT=xbcT[:, 24 + g, :], rhs=Hbf[:, gs], start=True, stop=True),
                   r=[xbcb, b_Hbf[g]], w=[pyob])
                pst, pstb = bank()
                pe(lambda e, pst=pst, g=g, Btok=Btok, xg=xg: e.matmul(pst[:, 0:256], lhsT=Btok[:, g * 128:(g + 1) * 128], rhs=xg[:, 1, :],
                                                                     start=True, stop=True), r=[Btokb, xgb], w=[pstb])
                y, yb_ = RY.next()
                dve(lambda e, y=y, po_g=po_g, dts=dts, hsl=hsl: e.tensor_tensor(out=y.rearrange("p (h c) -> p h c", h=4),
                                                                               in0=po_g.rearrange("p (h c) -> p h c", h=4),
                                                                               in1=dts[:, 4, hsl].unsqueeze(2).broadcast_to([128, 4, 64]),
                                                                               op=ALU.mult), r=[pyob, dtsb], w=[yb_])
                dve(lambda e, y=y, py_g=py_g: e.tensor_tensor(out=y, in0=py_g, in1=y, op=ALU.add), r=[pyob, yb_], w=[yb_])
                P.dma("sp", oloc_d[i][:, gs], y, reads=[yb_], writes=[b_yloc[i][g]], sembuf=yb_)
                if DBG == 24:
                    return
                pool(lambda e, gs=gs, dts=dts, hsl=hsl: e.tensor_tensor(out=H[:, gs].rearrange("p (h c) -> p h c", h=4),
                                                                       in0=H[:, gs].rearrange("p (h c) -> p h c", h=4),
                                                                       in1=dts[:, 6, hsl].unsqueeze(2).broadcast_to([128, 4, 64]),
                                                                       op=ALU.mult), r=[b_H[g], dtsb], w=[b_H[g]])
                dve(lambda e, gs=gs, pst=pst: e.tensor_tensor(out=H[:, gs], in0=H[:, gs], in1=pst[:, 0:256], op=ALU.add),
                    r=[b_H[g], pstb], w=[b_H[g]])
                act(lambda e, gs=gs: e.copy(out=Hbf[:, gs], in_=H[:, gs]), r=[b_H[g]], w=[b_Hbf[g]])
            if DBG == 2 and i == 0:
                return
            if i >= 8:
                kz = i - 8
                pool(lambda e, kz=kz: e.tensor_scalar(out=WBg[:, kz, :], in0=WBg[:, kz, :], scalar1=nwT1[:, kz:kz + 1], scalar2=None,
                                                     op0=ALU.mult), r=[b_WBg, b_sc], w=[b_WBg])
                for ko in (2 * kz, 2 * kz + 1):
                    pool(lambda e, ko=ko: e.tensor_scalar(out=WBo[:, ko, :], in0=WBo[:, ko, :], scalar1=gnT[:, ko:ko + 1], scalar2=None,
                                                         op0=ALU.mult), r=[b_WBo, b_sc], w=[b_WBo])

        if DBG == 3:
            return
        b_cc3in, b_cc3out, b_cc3 = Buf("cc3in"), Buf("cc3out"), Buf("cc3")
        b_dt3 = Buf("dtot3")
        act(lambda e: e.activation(out=dt3, in_=Ast, func=AF.Exp), r=[b_Ast], w=[b_dt3])
        P.dma("sp", cc3in_d[:, 0:2048], H, reads=b_H, writes=[b_cc3in], sembuf=b_cc3in)
        P.dma("sp", cc3in_d[:, 2048:2080], dt3, reads=[b_dt3], writes=[b_cc3in], sembuf=b_cc3in)
        P.custom_async("pool", lambda e: e.collective_compute("AllGather", ALU.bypass, replica_groups=[list(range(NCORES))],
                                                              ins=[cc3in_d], outs=[cc3out_d]), b_cc3, 1,
                       reads=[b_cc3in], writes=[b_cc3out])
        P.barrier()
        arena.reset()
        Hin = arena.alloc([128, 2048], F32)
        b_Hin = Buf("Hin")
        pool(lambda e: e.memset(Hin, 0.0), w=[b_Hin])
        RG3 = Ring(arena, "G3", [128, CCW], F32, 2)
        RDP3 = Ring(arena, "dp3", [128, 32], F32, 2)
        Hin_v = Hin.rearrange("p (h c) -> p h c", h=32)
        for j in range(NCORES - 1):
            G, Gb = RG3.next()
            P.dma("sp", G, cc3out_d[j * 128:(j + 1) * 128, :], reads=[b_cc3out], writes=[Gb])
            dp, dpb = RDP3.next()
            dve(lambda e, dp=dp, G=G, j=j: e.tensor_scalar(out=dp, in0=G[:, 2048:2080], scalar1=-1.0, scalar2=rmask[:, j:j + 1],
                                                          op0=ALU.add, op1=ALU.mult), r=[Gb, b_rmask], w=[dpb])
            dve(lambda e, dp=dp: e.tensor_scalar(out=dp, in0=dp, scalar1=1.0, scalar2=None, op0=ALU.add), r=[dpb], w=[dpb])
            dve(lambda e, dp=dp: e.tensor_tensor(out=Hin_v, in0=Hin_v, in1=dp.unsqueeze(2).broadcast_to([128, 32, 64]), op=ALU.mult),
                r=[b_Hin, dpb], w=[b_Hin])
            dve(lambda e, G=G, j=j: e.scalar_tensor_tensor(out=Hin, in0=G[:, 0:2048], scalar=rmask[:, j:j + 1], in1=Hin,
                                                          op0=ALU.mult, op1=ALU.add), r=[Gb, b_rmask, b_Hin], w=[b_Hin])
        act(lambda e: e.copy(out=Hinbf, in_=Hin), r=[b_Hin, b_Sinbf], w=[b_Hinbf])

        if DBG == 4:
            return
        P.barrier()
        arena.reset()
        setup_norm_rings()
        fnwb = arena.alloc([128, D], F32)
        b_fnwb = Buf("fnwb")
        P.dma("sp", fnwb, fnw_d.broadcast_to([128, D]), writes=[b_fnwb])
        RYL = Ring(arena, "yl", [128, 512], F32, 3)
        RCT = Ring(arena, "ct", [128, 8, 128], BF16, 2)
        REG = Ring(arena, "eg", [128, 32], F32, 2)
        RSZ = Ring(arena, "sz", [128, 512], F32, 2)
        RY2 = Ring(arena, "y2", [128, 512], F32, 2)
        RYB = Ring(arena, "yb", [128, 2048], BF16, 1)
        RYBT = Ring(arena, "ybT", [128, 16, 128], BF16, 1)
        RX2 = Ring(arena, "x2", [128, D], F32, 1)
        ROUT = Ring(arena, "outt", [128, D], F32, 1)
        RS3 = Ring(arena, "st3", [128, 16], F32, 2)
        junk3 = arena.alloc([128, 512], BF16)
        nt_next = norm_tile(x1_d[0:128, :], None, None)
        for i in range(NT):
            xt, xb, hnT, hTb, _st, _stb = nt_next
            if i + 1 < NT:
                nt_next = norm_tile(x1_d[(i + 1) * 128:(i + 2) * 128, :], None, None)
            ct, ctb = RCT.next()
            P.dma("sp", ct.rearrange("p g t -> p (g t)"), ct_d[i], reads=[b_ctd[i]], writes=[ctb])
            eg, egb = REG.next()
            P.dma("sp", eg, eag_d[i], reads=[b_eagd[i]], writes=[egb])
            yb, ybb = RYB.next()
            st3, s3b = RS3.next()
            for b in range(4):
                bs = slice(b * 512, (b + 1) * 512)
                yl, ylb = RYL.next()
                P.dma("sp", yl, oloc_d[i][:, bs], reads=[b_yloc[i][2 * b], b_yloc[i][2 * b + 1]], writes=[ylb])
                pz, pzb = bank()
                for k in range(8):
                    pe(lambda e, k=k, pz=pz, hnT=hnT, bs=bs: e.matmul(pz, lhsT=hnT[:, k, :], rhs=WBg[:, k, bs], start=(k == 0),
                                                                     stop=(k == 7)), r=[hTb, b_WBg], w=[pzb])
                sz, szb = RSZ.next()
                act(lambda e, sz=sz, pz=pz: e.activation(out=sz, in_=pz, func=AF.Silu), r=[pzb], w=[szb])
                pc, pcb_ = bank()
                for gg in range(2):
                    g = 2 * b + gg
                    pe(lambda e, gg=gg, g=g, pc=pc, ct=ct: e.matmul(pc[:, gg * 256:(gg + 1) * 256], lhsT=ct[:, g, :],
                                                                   rhs=Hinbf[:, g * 256:(g + 1) * 256], start=True, stop=True),
                       r=[ctb, b_Hinbf], w=[pcb_])
                y2, y2b = RY2.next()
                dve(lambda e, y2=y2, pc=pc, eg=eg, b=b: e.tensor_tensor(out=y2.rearrange("p (h c) -> p h c", h=8),
                                                                       in0=pc.rearrange("p (h c) -> p h c", h=8),
                                                                       in1=eg[:, 8 * b:8 * b + 8].unsqueeze(2).broadcast_to([128, 8, 64]),
                                                                       op=ALU.mult), r=[pcb_, egb], w=[y2b])
                pool(lambda e, y2=y2, yl=yl: e.tensor_tensor(out=y2, in0=y2, in1=yl, op=ALU.add), r=[y2b, ylb], w=[y2b])
                dve(lambda e, y2=y2, sz=sz: e.tensor_tensor(out=y2, in0=y2, in1=sz, op=ALU.mult), r=[y2b, szb], w=[y2b])
                act(lambda e, y2=y2, st3=st3, b=b: e.activation(out=junk3, in_=y2, func=AF.Square, accum_out=st3[:, b:b + 1]),
                    r=[y2b], w=[s3b])
                pool(lambda e, yb=yb, y2=y2, bs=bs: e.tensor_copy(out=yb[:, bs], in_=y2), r=[y2b], w=[ybb])
            dve(lambda e, st3=st3: e.tensor_tensor(out=st3[:, 4:6], in0=st3[:, 0:2], in1=st3[:, 2:4], op=ALU.add), r=[s3b], w=[s3b])
            dve(lambda e, st3=st3: e.tensor_tensor(out=st3[:, 6:7], in0=st3[:, 4:5], in1=st3[:, 5:6], op=ALU.add), r=[s3b], w=[s3b])
            act(lambda e, st3=st3: e.activation(out=st3[:, 7:8], in_=st3[:, 6:7], func=AF.Ln, scale=1.0 / 2048, bias=EPS), r=[s3b], w=[s3b])
            act(lambda e, st3=st3: e.activation(out=st3[:, 8:9], in_=st3[:, 7:8], func=AF.Exp, scale=-0.5), r=[s3b], w=[s3b])
            ybT, ybTb = RYBT.next()
            for half in range(2):
                pt3, pt3b = bank()
                pt3b16 = pt3.bitcast(BF16)
                for j in range(8):
                    kk = half * 8 + j
                    pe(lambda e, j=j, kk=kk, yb=yb, pt3b16=pt3b16: e.transpose(out=pt3b16[:, j * 128:(j + 1) * 128],
                                                                              in_=yb[:, kk * 128:(kk + 1) * 128], identity=ident),
                       r=[ybb, b_ident], w=[pt3b])
                act(lambda e, half=half, ybT=ybT, pt3b16=pt3b16: e.copy(
                    out=ybT[:, half * 8:(half + 1) * 8, :].rearrange("p k t -> p (k t)"), in_=pt3b16), r=[pt3b], w=[ybTb])
            x2, x2b = RX2.next()
            for c in range(2):
                cs = slice(c * 512, (c + 1) * 512)
                py, pyb = bank()
                for k in range(16):
                    pe(lambda e, k=k, py=py, ybT=ybT, cs=cs: e.matmul(py, lhsT=ybT[:, k, :], rhs=WBo[:, k, cs], start=(k == 0),
                                                                     stop=(k == 15)), r=[ybTb, b_WBo], w=[pyb])
                dve(lambda e, x2=x2, py=py, xt=xt, cs=cs, st3=st3: e.scalar_tensor_tensor(out=x2[:, cs], in0=py, scalar=st3[:, 8:9],
                                                                                         in1=xt[:, cs], op0=ALU.mult, op1=ALU.add),
                    r=[pyb, s3b, xb], w=[x2b])
            outt, outb = ROUT.next()
            act(lambda e, x2=x2, st3=st3, outt=outt: e.activation(out=outt, in_=x2, func=AF.Square, accum_out=st3[:, 9:10]), r=[x2b], w=[s3b, outb])
            act(lambda e, st3=st3: e.activation(out=st3[:, 10:11], in_=st3[:, 9:10], func=AF.Ln, scale=1.0 / D, bias=EPS), r=[s3b], w=[s3b])
            act(lambda e, st3=st3: e.activation(out=st3[:, 11:12], in_=st3[:, 10:11], func=AF.Exp, scale=-0.5), r=[s3b], w=[s3b])
            dve(lambda e, outt=outt, x2=x2, st3=st3: e.scalar_tensor_tensor(out=outt, in0=x2, scalar=st3[:, 11:12], in1=fnwb,
                                                                           op0=ALU.mult, op1=ALU.mult), r=[x2b, s3b, b_fnwb], w=[outb])
            P.dma("sp", out_d[i * 128:(i + 1) * 128, :], outt, reads=[outb], sembuf=outb)


    if stage >= 2:
        emit_l1()
    P.final_wait("sp", [o for o in P.ops if o.is_dma])
    P.emit()
    return nc


def make_in_maps(inputs):
    x = np.ascontiguousarray(np.asarray(inputs["x"], dtype=np.float32).reshape(SEQ, D))
    maps = []
    for c in range(NCORES):
        rm = np.zeros((128, 16), np.float32)
        rm[:, :c] = 1.0
        if c > 0:
            rm[:, 8 + c - 1] = 1.0
        m = {
            "x": x[c * T:(c + 1) * T],
            "norm_w": np.asarray(inputs["norm_w"], np.float32),
            "gla_in_proj": np.asarray(inputs["gla_in_proj"], np.float32)[0],
            "gla_gate_up": np.asarray(inputs["gla_gate_up"], np.float32)[0],
            "gla_gate_bias": np.asarray(inputs["gla_gate_bias"], np.float32).reshape(1, 512),
            "gla_head_norm": np.asarray(inputs["gla_head_norm"], np.float32).reshape(1, 512),
            "gla_out_proj": np.asarray(inputs["gla_out_proj"], np.float32)[0],
            "rmask": rm,
            "ssd_in_proj": np.asarray(inputs["ssd_in_proj"], np.float32)[0],
            "ssd_conv_w": np.asarray(inputs["ssd_conv_w"], np.float32)[0],
            "ssd_conv_b": np.asarray(inputs["ssd_conv_b"], np.float32).reshape(1, 4096),
            "ssd_dt_bias": np.asarray(inputs["ssd_dt_bias"], np.float32).reshape(1, 32),
            "ssd_a_log": np.asarray(inputs["ssd_a_log"], np.float32).reshape(1, 32),
            "ssd_d": np.asarray(inputs["ssd_d"], np.float32).reshape(1, 32),
            "ssd_gate_norm": np.asarray(inputs["ssd_gate_norm"], np.float32).reshape(16, 128),
            "ssd_out_proj": np.asarray(inputs["ssd_out_proj"], np.float32)[0],
            "final_norm": np.asarray(inputs["final_norm"], np.float32).reshape(1, D),
        }
        maps.append(m)
    return maps


def kernel(**inputs):
    nc = build(stage=STAGE)
    maps = make_in_maps(inputs)
    res = run_bass_kernel_spmd(nc, maps, core_ids=list(range(NCORES)))
    if DBG:
        global LAST_RES
        LAST_RES = res
    out = np.concatenate([res.results[c]["out"] for c in range(NCORES)], axis=0)
    return out.reshape(1, SEQ, D).astype(np.float32)
```
